# Optimizing a Trainium2 kernel written in Bass

```python
import math
import jax, jax.numpy as jnp
from jax import lax
import numpy as np


D_MODEL = 2048
BATCH = 2
SEQ = 4096
DEPTH = 2

EPS = 1e-6
GRID_W = 64
HEAD_DIM = 128
MIX_WIDTH = D_MODEL
N_MIX_HEADS = MIX_WIDTH // HEAD_DIM
NA_HEADS = N_MIX_HEADS // 4
NA_KH = 8
NA_KW = 16
GQA_HEADS = N_MIX_HEADS // 2
GQA_KV = GQA_HEADS // 4
GQA_GROUP = GQA_HEADS // GQA_KV
AXIAL_DIM = HEAD_DIM // 2
DIFF_HEADS = N_MIX_HEADS // 4
DIFF_QK_DIM = HEAD_DIM // 2
DIFF_V_DIM = HEAD_DIM
A_COLS = 3 * NA_HEADS * HEAD_DIM
B_COLS = (GQA_HEADS + 2 * GQA_KV) * HEAD_DIM
C_COLS = DIFF_HEADS * (2 * 2 * DIFF_QK_DIM + DIFF_V_DIM)
IN_COLS = A_COLS + B_COLS + C_COLS
OUT_COLS = NA_HEADS * HEAD_DIM + GQA_HEADS * HEAD_DIM + DIFF_HEADS * DIFF_V_DIM
D_FF = 5632
PLE_DIM = 256
Q_BLOCK = 128
ROPE_THETA = 10000.0

kernel_name = "hybrid_na_gqa_diff_macaron_encoder"


def _rmsnorm(x, g):
    xf = x.astype(jnp.float32)
    y = xf * lax.rsqrt(jnp.mean(xf * xf, axis=-1, keepdims=True) + EPS)
    return (y * g.astype(jnp.float32)).astype(x.dtype)


def _swiglu(x, wg, wu, wd):
    return (jax.nn.silu(x @ wg) * (x @ wu)) @ wd


def _rope(x, cos, sin):
    half = x.shape[-1] // 2
    x1, x2 = x[..., :half], x[..., half:]
    return jnp.concatenate([x1 * cos - x2 * sin, x2 * cos + x1 * sin], axis=-1)


def _neighbourhood_attention(q, k, v, rpb, rows):
    B, S, H, d = q.shape
    kh = min(NA_KH, rows)
    kw = NA_KW
    qg = (q * (d ** -0.5)).reshape(B, rows, GRID_W, H, d)
    kg = k.reshape(B, rows, GRID_W, H, d)
    vg = v.reshape(B, rows, GRID_W, H, d)
    row_start = jnp.clip(jnp.arange(rows) - kh // 2, 0, rows - kh)
    col_idx = jnp.clip(jnp.arange(GRID_W) - kw // 2, 0, GRID_W - kw)[:, None] + jnp.arange(kw)[None, :]
    dc = col_idx - jnp.arange(GRID_W)[:, None] + (NA_KW - 1)

    def one_row(r):
        rs = row_start[r]
        q_r = lax.dynamic_index_in_dim(qg, r, axis=1, keepdims=False)
        k_win = lax.dynamic_slice_in_dim(kg, rs, kh, axis=1)[:, :, col_idx]
        v_win = lax.dynamic_slice_in_dim(vg, rs, kh, axis=1)[:, :, col_idx]
        dr = rs + jnp.arange(kh) - r + (NA_KH - 1)
        bias = rpb[:, dr[None, :, None], dc[:, None, :]]
        s = jnp.einsum('bchd,bicjhd->bhcij', q_r, k_win).astype(jnp.float32) + bias.astype(jnp.float32)
        pr = jax.nn.softmax(s.reshape(B, H, GRID_W, kh * kw), axis=-1).reshape(B, H, GRID_W, kh, kw)
        return jnp.einsum('bhcij,bicjhd->bchd', pr.astype(v.dtype), v_win)

    out = lax.map(one_row, jnp.arange(rows))
    return jnp.moveaxis(out, 0, 1).reshape(B, S, H * d)


def _gqa_axial_attention(q, k, v, q_gain, k_gain, cos, sin):
    B, S, _, d = q.shape
    nb = S // Q_BLOCK
    q = _rope(_rmsnorm(q, q_gain), cos, sin)
    k = _rope(_rmsnorm(k, k_gain), cos, sin)
    qh = q.reshape(B, S, GQA_KV, GQA_GROUP, d).transpose(0, 2, 3, 1, 4)
    kh = k.transpose(0, 2, 1, 3)
    vh = v.transpose(0, 2, 1, 3)
    qb = jnp.moveaxis(qh.reshape(B, GQA_KV, GQA_GROUP, nb, Q_BLOCK, d), 3, 0)
    scale = d ** -0.5

    def blk(qi):
        s = jnp.einsum('bkgqd,bksd->bkgqs', qi, kh).astype(jnp.float32) * scale
        pr = jax.nn.softmax(s, axis=-1)
        return jnp.einsum('bkgqs,bksd->bkgqd', pr.astype(v.dtype), vh)

    o = lax.map(blk, qb)
    o = jnp.moveaxis(o, 0, 3).reshape(B, GQA_KV, GQA_GROUP, S, d)
    return o.transpose(0, 3, 1, 2, 4).reshape(B, S, GQA_HEADS * d)


def _diff_attention(q, k, v, lam_params, subln_gain, cos, sin, layer):
    B, S, _ = q.shape
    nb = S // Q_BLOCK
    q = _rope(q.reshape(B, S, DIFF_HEADS, 2, DIFF_QK_DIM), cos, sin)
    k = _rope(k.reshape(B, S, DIFF_HEADS, 2, DIFF_QK_DIM), cos, sin)
    v = v.reshape(B, S, DIFF_HEADS, DIFF_V_DIM)
    lam_init = 0.8 - 0.6 * math.exp(-0.3 * layer)
    lp = lam_params.astype(jnp.float32)
    lam = jnp.exp(jnp.sum(lp[0] * lp[1])) - jnp.exp(jnp.sum(lp[2] * lp[3])) + lam_init
    qh = q.transpose(0, 2, 3, 1, 4)
    kh = k.transpose(0, 2, 3, 1, 4)
    vh = v.transpose(0, 2, 1, 3)
    qb = jnp.moveaxis(qh.reshape(B, DIFF_HEADS, 2, nb, Q_BLOCK, DIFF_QK_DIM), 3, 0)
    scale = DIFF_QK_DIM ** -0.5

    def blk(qi):
        s = jnp.einsum('bhcqd,bhcsd->bhcqs', qi, kh).astype(jnp.float32) * scale
        pr = jax.nn.softmax(s, axis=-1)
        a = pr[:, :, 0] - lam * pr[:, :, 1]
        return jnp.einsum('bhqs,bhsd->bhqd', a.astype(v.dtype), vh)

    o = lax.map(blk, qb)
    o = jnp.moveaxis(o, 0, 2).reshape(B, DIFF_HEADS, S, DIFF_V_DIM)
    o = _rmsnorm(o, subln_gain) * (1.0 - lam_init)
    return o.transpose(0, 2, 1, 3).reshape(B, S, DIFF_HEADS * DIFF_V_DIM)


def setup_inputs(seed: int = 0) -> dict:
    key = jax.random.key(seed)
    ks = jax.random.split(key, 24)
    f32 = jnp.float32

    def nrm(k, shape, scale):
        return jax.random.normal(k, shape, f32) * scale

    def gain(k, shape):
        return 1.0 + 0.01 * jax.random.normal(k, shape, f32)

    return {
        "x": nrm(ks[0], (BATCH, SEQ, D_MODEL), 1.0),
        "p": nrm(ks[1], (DEPTH, BATCH, SEQ, PLE_DIM), 1.0),
        "g_ffn1": gain(ks[2], (DEPTH, D_MODEL)),
        "w1_gate": nrm(ks[3], (DEPTH, D_MODEL, D_FF), D_MODEL ** -0.5),
        "w1_up": nrm(ks[4], (DEPTH, D_MODEL, D_FF), D_MODEL ** -0.5),
        "w1_down": nrm(ks[5], (DEPTH, D_FF, D_MODEL), D_FF ** -0.5),
        "g_mix": gain(ks[6], (DEPTH, D_MODEL)),
        "w_in": nrm(ks[7], (DEPTH, D_MODEL, IN_COLS), D_MODEL ** -0.5),
        "na_rpb": nrm(ks[8], (DEPTH, NA_HEADS, 2 * NA_KH - 1, 2 * NA_KW - 1), 0.1),
        "gqa_q_gain": gain(ks[9], (DEPTH, HEAD_DIM)),
        "gqa_k_gain": gain(ks[10], (DEPTH, HEAD_DIM)),
        "diff_lambda": nrm(ks[11], (DEPTH, 4, DIFF_QK_DIM), 0.1),
        "diff_subln_gain": gain(ks[12], (DEPTH, DIFF_V_DIM)),
        "w_out": nrm(ks[13], (DEPTH, OUT_COLS, D_MODEL), OUT_COLS ** -0.5),
        "g_ffn2": gain(ks[14], (DEPTH, D_MODEL)),
        "w2_gate": nrm(ks[15], (DEPTH, D_MODEL, D_FF), D_MODEL ** -0.5),
        "w2_up": nrm(ks[16], (DEPTH, D_MODEL, D_FF), D_MODEL ** -0.5),
        "w2_down": nrm(ks[17], (DEPTH, D_FF, D_MODEL), D_FF ** -0.5),
        "g_ple": gain(ks[18], (DEPTH, D_MODEL)),
        "w_ple_gate": nrm(ks[19], (DEPTH, D_MODEL, D_MODEL), D_MODEL ** -0.5),
        "w_ple_proj": nrm(ks[20], (DEPTH, PLE_DIM, D_MODEL), PLE_DIM ** -0.5),
        "g_final": gain(ks[21], (D_MODEL,)),
    }


def reference(x, p, g_ffn1, w1_gate, w1_up, w1_down, g_mix, w_in, na_rpb, gqa_q_gain, gqa_k_gain,
              diff_lambda, diff_subln_gain, w_out, g_ffn2, w2_gate, w2_up, w2_down, g_ple,
              w_ple_gate, w_ple_proj, g_final):
    B, S, _ = x.shape
    rows = S // GRID_W
    f32 = jnp.float32
    t = jnp.arange(S)
    inv_ax = jnp.power(ROPE_THETA, -jnp.arange(0, AXIAL_DIM, 2, dtype=f32) / AXIAL_DIM)
    ang_b = jnp.concatenate([(t // GRID_W).astype(f32)[:, None] * inv_ax,
                             (t % GRID_W).astype(f32)[:, None] * inv_ax], axis=-1)
    cos_b = jnp.cos(ang_b)[:, None, :].astype(x.dtype)
    sin_b = jnp.sin(ang_b)[:, None, :].astype(x.dtype)
    inv_1d = jnp.power(ROPE_THETA, -jnp.arange(0, DIFF_QK_DIM, 2, dtype=f32) / DIFF_QK_DIM)
    ang_c = t.astype(f32)[:, None] * inv_1d
    cos_c = jnp.cos(ang_c)[:, None, None, :].astype(x.dtype)
    sin_c = jnp.sin(ang_c)[:, None, None, :].astype(x.dtype)

    h = x
    for i in range(DEPTH):
        h = h + 0.5 * _swiglu(_rmsnorm(h, g_ffn1[i]), w1_gate[i], w1_up[i], w1_down[i])
        proj = _rmsnorm(h, g_mix[i]) @ w_in[i]
        a_part, b_part, c_part = jnp.split(proj, [A_COLS, A_COLS + B_COLS], axis=-1)
        qa, ka, va = [u.reshape(B, S, NA_HEADS, HEAD_DIM) for u in jnp.split(a_part, 3, axis=-1)]
        qb_, kb_, vb_ = jnp.split(b_part, [GQA_HEADS * HEAD_DIM, (GQA_HEADS + GQA_KV) * HEAD_DIM], axis=-1)
        qb_ = qb_.reshape(B, S, GQA_HEADS, HEAD_DIM)
        kb_ = kb_.reshape(B, S, GQA_KV, HEAD_DIM)
        vb_ = vb_.reshape(B, S, GQA_KV, HEAD_DIM)
        qc, kc, vc = jnp.split(c_part, [DIFF_HEADS * 2 * DIFF_QK_DIM, DIFF_HEADS * 4 * DIFF_QK_DIM], axis=-1)
        out_a = _neighbourhood_attention(qa, ka, va, na_rpb[i], rows)
        out_b = _gqa_axial_attention(qb_, kb_, vb_, gqa_q_gain[i], gqa_k_gain[i], cos_b, sin_b)
        out_c = _diff_attention(qc, kc, vc, diff_lambda[i], diff_subln_gain[i], cos_c, sin_c, i)
        h = h + jnp.concatenate([out_a, out_b, out_c], axis=-1) @ w_out[i]
        h = h + 0.5 * _swiglu(_rmsnorm(h, g_ffn2[i]), w2_gate[i], w2_up[i], w2_down[i])
        gate = jax.nn.sigmoid(_rmsnorm(h, g_ple[i]) @ w_ple_gate[i])
        h = h + gate * (p[i] @ w_ple_proj[i])
    return _rmsnorm(h, g_final)
```

```python
import math
import numpy as np
import ml_dtypes
import concourse.bass as bass
import concourse.mybir as mybir
from concourse.bass_utils import run_bass_kernel_spmd

F32 = mybir.dt.float32
BF16 = mybir.dt.bfloat16
AF = mybir.ActivationFunctionType
ALU = mybir.AluOpType
AX = mybir.AxisListType

D = 2048
DFF = 5632
NTOK = 1024
S = 4096
EPS = 1e-6
NEG = -30000.0
NFG = 4
FG = 11
NWS = 4

CV_G1, CV_GM, CV_G2, CV_GP, CV_GF = 0, 16, 32, 48, 64
CV_QG, CV_KG, CV_SUB = 80, 81, 82
CV_LAM = 83
NCV = CV_LAM + 256

PROJ = ([(c, "Aq") for c in range(0, 4)] + [(c, "Ak") for c in range(4, 8)] +
        [(c, "Bq") for c in range(12, 20)] + [(c, "Bk") for c in range(20, 22)] +
        [(c, "Cq") for c in range(24, 28)] + [(c, "Ck") for c in range(28, 32)])
VGROUPS = [(1024, 512, 0), (2816, 256, 512), (4096, 512, 768)]


class _Sem:
    def __init__(self, h):
        self.h = h
        self.n = 0


class Prog:
    ENG = ("pe", "act", "dve", "pool", "sp")

    def __init__(self, nc):
        self.nc = nc
        self.sems = []
        self.ops = {e: [] for e in self.ENG}
        self.esem = {e: self.sem("e_" + e) for e in ("pe", "act", "dve", "pool")}
        self.lastw = {}
        self.rd = {}
        self.waited = {e: {} for e in self.ENG}
        self.rings = {}

    def sem(self, name):
        s = _Sem(self.nc.alloc_semaphore(name))
        self.sems.append(s)
        return s

    def op(self, eng, fn, r=(), w=(), dsem=None):
        need = {}

        def add(ev):
            s, v = ev
            if need.get(s, 0) < v:
                need[s] = v
        for k in r:
            if k in self.lastw:
                add(self.lastw[k])
        for k in w:
            if k in self.lastw:
                add(self.lastw[k])
            for s, v in self.rd.get(k, {}).items():
                add((s, v))
        wl = []
        wd = self.waited[eng]
        for s, v in need.items():
            if eng == "pe" and s is self.esem["pe"]:
                continue
            if wd.get(s, 0) < v:
                wd[s] = v
                wl.append((s, v))
        if dsem is None:
            s = self.esem[eng]
            s.n += 1
            ev = (s, s.n)
            inc = (s, 1)
        else:
            dsem.n += 16
            ev = (dsem, dsem.n)
            inc = (dsem, 16)
        self.ops[eng].append((fn, wl, inc))
        for k in r:
            d = self.rd.setdefault(k, {})
            if d.get(ev[0], 0) < ev[1]:
                d[ev[0]] = ev[1]
        for k in w:
            self.lastw[k] = ev
            self.rd[k] = {}
        return ev

    def barrier(self, engs=None):
        for e in (engs or self.ENG):
            wl = []
            wd = self.waited[e]
            for s in self.sems:
                if s.n > 0 and wd.get(s, 0) < s.n:
                    wd[s] = s.n
                    wl.append((s, s.n))
            if wl:
                self.ops[e].append((None, wl, None))
        if engs is None:
            self.lastw = {}
            self.rd = {}

    def ring(self, name, items):
        self.rings[name] = [items, 0]

    def nxt(self, name):
        r = self.rings[name]
        it = r[0][r[1] % len(r[0])]
        r[1] += 1
        return it

    def emit(self):
        nc = self.nc
        with nc.Block() as block:
            def mk(e):
                def f(eng):
                    for fn, wl, inc in self.ops[e]:
                        for s, v in wl:
                            eng.wait_ge(s.h, v)
                        if fn is None:
                            continue
                        ins = fn(eng)
                        ins.then_inc(inc[0].h, inc[1])
                return f
            block.tensor(mk("pe"))
            block.scalar(mk("act"))
            block.vector(mk("dve"))
            block.gpsimd(mk("pool"))
            block.sync(mk("sp"))


class Arena:
    def __init__(self, nc, nbytes):
        self.t = nc.alloc_sbuf_tensor("arena", [128, nbytes // 2], BF16)
        self.cap = nbytes
        self.off = 0

    def alloc(self, free_shape, dtype):
        es = 4 if dtype == F32 else 2
        n = int(np.prod(free_shape)) * es
        self.off = (self.off + 63) // 64 * 64
        a = self.off
        assert a + n <= self.cap, ("arena overflow", a, n, self.cap)
        self.off = a + n
        ap = self.t[:, a // 2:(a + n) // 2]
        if dtype == F32:
            ap = ap.bitcast(F32)
        if len(free_shape) == 2:
            ap = ap.rearrange("p (a b) -> p a b", a=free_shape[0])
        elif len(free_shape) == 3:
            ap = ap.rearrange("p (a b c) -> p a b c", a=free_shape[0], b=free_shape[1])
        return ap

    def mark(self):
        return self.off

    def release(self, m):
        self.off = m


class Ctx:
    def __init__(self):
        nc = bass.Bass("TRN2", target_bir_lowering=False)
        self.nc = nc
        self.P = Prog(nc)
        self.A = Arena(nc, 206 * 1024)
        P, A = self.P, self.A
        self.h = A.alloc((16, NTOK), F32)
        self.xn = A.alloc((16, NTOK), BF16)
        self.cv = A.alloc((NCV,), F32)
        self.ones = A.alloc((128,), BF16)
        self.perm = A.alloc((2, 128), BF16)
        self.misc = A.alloc((16,), F32)
        ws = [A.alloc((16, 128), BF16) for _ in range(NWS)]
        P.ring("w", [(ws[i], ("w", i), P.sem("w%d" % i)) for i in range(NWS)])
        P.ring("tf", [(A.alloc((512,), F32), ("tf", i)) for i in range(6)])
        P.ring("tb", [(A.alloc((512,), BF16), ("tb", i)) for i in range(4)])
        self.pst = [nc.alloc_psum_tensor("ps%d" % i, [128, 1024], F32) for i in range(4)]
        P.ring("ps", [(self.pst[i // 2][:, (i % 2) * 512:(i % 2 + 1) * 512], ("ps", i)) for i in range(8)])
        self.nsem = 0
        self.out_sem = P.sem("out")
        self.out_sems = [self.out_sem]
        self.base = A.mark()

    def dram_in(self, name, shape, dtype=F32):
        return self.nc.dram_tensor(name, list(shape), dtype, kind="ExternalInput").ap()

    def dram_out(self, name, shape, dtype=F32):
        return self.nc.dram_tensor(name, list(shape), dtype, kind="ExternalOutput").ap()

    def load(self, dst, src, key, eng="sp", sem=None):
        if sem is None:
            self.nsem += 1
            sem = self.P.sem("ld%d" % self.nsem)
        keys = key if isinstance(key, list) else [key]
        self.P.op(eng, lambda e: e.dma_start(out=dst, in_=src), w=keys, dsem=sem)

    def store(self, dst, src, rkeys, key, sem=None):
        sem = sem or self.out_sem
        self.P.op("sp", lambda e: e.dma_start(out=dst, in_=src), r=rkeys, w=[key], dsem=sem)
        if sem not in self.out_sems:
            self.out_sems.append(sem)

    def setup_consts(self, cv_d, perm_d):
        P = self.P
        self.load(self.cv, cv_d, "cv")
        self.load(self.perm, perm_d.rearrange("a p n -> p a n"), "perm", eng="pool")
        P.op("dve", lambda e: e.memset(self.ones, 1.0), w=["ones"])

    def load_h(self, h_d):
        self.load(self.h, h_d.rearrange("(k p) t -> p k t", p=128), [("h", k) for k in range(16)])

    def hkeys(self, k, t):
        return [("h", k)]

    def wload(self, W, r0, nk, c0):
        slot, key, sem = self.P.nxt("w")
        dst = slot[:, 0:nk, :]
        src = W[r0:r0 + nk * 128, c0:c0 + 128].rearrange("(k p) n -> p k n", p=128)
        self.P.op("pool", lambda e: e.dma_start(out=dst, in_=src), w=[key], dsem=sem)
        return slot, key

    def rstd_from_ss(self, ss_ap, ss_key, n):
        P = self.P
        v, vk = P.nxt("tf")
        P.op("dve", lambda e: e.tensor_scalar(out=v, in0=ss_ap, scalar1=1.0 / n, scalar2=EPS,
                                              op0=ALU.mult, op1=ALU.add), r=[ss_key], w=[vk])
        P.op("act", lambda e: e.activation(out=v, in_=v, func=AF.Sqrt), r=[vk], w=[vk])
        rs, rk = P.nxt("tf")
        P.op("dve", lambda e: e.reciprocal(out=rs, in_=v), r=[vk], w=[rk])
        return rs, rk

    def rmsnorm(self, gcol, out, okey, store_fn=None):
        P = self.P
        h, cv = self.h, self.cv
        def body(t):
            ts = slice(t * 512, (t + 1) * 512)
            ss, ssk = P.nxt("ps")
            for k in range(16):
                sq, sqk = P.nxt("tb")
                P.op("act", lambda e, sq=sq, k=k: e.activation(out=sq, in_=h[:, k, ts], func=AF.Square),
                     r=[("h", k)], w=[sqk])
                P.op("pe", lambda e, sq=sq, k=k: e.matmul(ss, lhsT=self.ones, rhs=sq, start=(k == 0), stop=(k == 15)),
                     r=[sqk, "ones"], w=[ssk])
            rs, rk = self.rstd_from_ss(ss, ssk, D)
            for k in range(16):
                if store_fn is None:
                    o, ok = out[:, k, ts], (okey, k, t)
                else:
                    o, ok = P.nxt("of")
                P.op("dve", lambda e, k=k, o=o: e.scalar_tensor_tensor(
                    out=o, in0=h[:, k, ts], scalar=cv[:, gcol + k:gcol + k + 1], in1=rs,
                    op0=ALU.mult, op1=ALU.mult), r=[("h", k), rk, "cv"], w=[ok])
                if store_fn is not None:
                    store_fn(k, t, ts, o, ok)
        for t in range(2):
            body(t)

    def linear(self, steps, nk, rhs_fn, epi):
        P = self.P
        for si, st in enumerate(steps):
            wts = [self.wload(W, r0, nk, c0) for (W, r0, c0) in st]
            for t in range(2):
                outs = []
                for (wt, wk) in wts:
                    ps, pk = P.nxt("ps")
                    rr = [rhs_fn(k, t) for k in range(nk)]

                    def mm(e, wt=wt, ps=ps, rr=rr):
                        ins = None
                        for k in range(nk):
                            ins = e.matmul(ps, lhsT=wt[:, k, :], rhs=rr[k][0], start=(k == 0), stop=(k == nk - 1))
                        return ins
                    P.op("pe", mm, r=[wk] + [x[1] for x in rr], w=[pk])
                    outs.append((ps, pk))
                epi(si, t, outs)

    def ffn(self, gcol, wg, wu, wd):
        P, A = self.P, self.A
        m = A.mark()
        aT = A.alloc((FG, NTOK), BF16)
        self.rmsnorm(gcol, self.xn, "xn")
        xn, h = self.xn, self.h
        for g in range(NFG):
            def epi_a(si, t, outs, g=g):
                ts = slice(t * 512, (t + 1) * 512)
                (gp, gk), (up, uk) = outs
                sg, sk = P.nxt("tf")
                P.op("act", lambda e: e.activation(out=sg, in_=gp, func=AF.Silu), r=[gk], w=[sk])
                P.op("dve", lambda e: e.tensor_tensor(out=aT[:, si, ts], in0=up, in1=sg, op=ALU.mult),
                     r=[uk, sk], w=[("aT", si, t)])
            steps = [[(wg, 0, (g * FG + j) * 128), (wu, 0, (g * FG + j) * 128)] for j in range(FG)]
            self.linear(steps, 16, lambda k, t: (xn[:, k, t * 512:(t + 1) * 512], ("xn", k, t)), epi_a)

            def epi_b(si, t, outs):
                ts = slice(t * 512, (t + 1) * 512)
                (yp, yk), = outs
                P.op("dve", lambda e: e.scalar_tensor_tensor(out=h[:, si, ts], in0=yp, scalar=0.5, in1=h[:, si, ts],
                                                             op0=ALU.mult, op1=ALU.add),
                     r=[yk, ("h", si)], w=[("h", si)])
            steps = [[(wd, g * FG * 128, dc * 128)] for dc in range(16)]
            self.linear(steps, FG, lambda k, t: (aT[:, k, t * 512:(t + 1) * 512], ("aT", k, t)), epi_b)
        P.barrier()
        A.release(m)


def build_stage_a():
    C = Ctx()
    nc, P, A = C.nc, C.P, C.A
    h_d = C.dram_in("hT", [D, NTOK])
    wg = C.dram_in("w_gate", [D, DFF])
    wu = C.dram_in("w_up", [D, DFF])
    wd = C.dram_in("w_down", [DFF, D])
    win = C.dram_in("w_in", [D, 4608])
    cv_d = C.dram_in("cv", [128, NCV])
    perm_d = C.dram_in("perm", [2, 128, 128])
    rope_d = C.dram_in("rope", [4, 128, NTOK])
    h_o = C.dram_out("hT_out", [D, NTOK])
    qk_o = C.dram_out("qk_out", [len(PROJ), 128, NTOK], BF16)
    v_o = C.dram_out("v_out", [NTOK, 1280], BF16)

    C.setup_consts(cv_d, perm_d)
    C.load_h(h_d)
    C.ffn(CV_G1, wg, wu, wd)
    for k in range(16):
        C.store(h_o[k * 128:(k + 1) * 128, :], C.h[:, k, :], [("h", k)], ("h_o", k))
    emit_proj(C, win, rope_d, qk_o, v_o)
    finish(C)
    return nc


def finish(C):
    P = C.P
    P.ops["sp"].append((None, [(s, s.n) for s in C.out_sems if s.n > 0], None))
    P.emit()


def emit_proj(C, win, rope_d, qk_o, v_o):
    P, A = C.P, C.A
    m = A.mark()
    rope = A.alloc((4, NTOK), F32)
    wv = A.alloc((16, 512), BF16)
    wv_sem = P.sem("wv")
    qs = [A.alloc((NTOK,), BF16) for _ in range(3)]
    qsem = [P.sem("qs%d" % i) for i in range(3)]
    P.ring("qs", [(qs[i], ("qs", i), qsem[i]) for i in range(3)])
    vs = [A.alloc((512,), BF16) for _ in range(2)]
    vsem = [P.sem("vs%d" % i) for i in range(2)]
    P.ring("vs", [(vs[i], ("vs", i), vsem[i]) for i in range(2)])
    C.load(rope, rope_d.rearrange("a p n -> p a n"), "rope")
    C.rmsnorm(CV_GM, C.xn, "xn")
    xn, cv = C.xn, C.cv
    cur = {}

    def epi(si, t, outs):
        ci, kind = PROJ[si]
        ts = slice(t * 512, (t + 1) * 512)
        (ps, pk), = outs
        if t == 0:
            cur["q"] = P.nxt("qs")
        qsb, qk, qsm = cur["q"]
        okey = (qk, t)
        dst = qsb[:, ts]
        if kind == "Aq":
            P.op("act", lambda e: e.activation(out=dst, in_=ps, func=AF.Copy, scale=128.0 ** -0.5), r=[pk], w=[okey, qk])
        elif kind == "Ak":
            P.op("act", lambda e: e.activation(out=dst, in_=ps, func=AF.Copy), r=[pk], w=[okey, qk])
        else:
            xb, xk = P.nxt("tb")
            if kind[0] == "B":
                sq, sqk = P.nxt("tb")
                P.op("act", lambda e: e.activation(out=sq, in_=ps, func=AF.Square), r=[pk], w=[sqk])
                ss, ssk = P.nxt("ps")
                P.op("pe", lambda e: e.matmul(ss, lhsT=C.ones, rhs=sq, start=True, stop=True), r=[sqk, "ones"], w=[ssk])
                rs, rk = C.rstd_from_ss(ss, ssk, 128)
                gc = CV_QG if kind == "Bq" else CV_KG
                P.op("dve", lambda e: e.scalar_tensor_tensor(out=xb, in0=ps, scalar=cv[:, gc:gc + 1], in1=rs,
                                                             op0=ALU.mult, op1=ALU.mult), r=[pk, rk, "cv"], w=[xk])
                pm, ct, st_ = 0, 0, 1
            else:
                P.op("act", lambda e: e.activation(out=xb, in_=ps, func=AF.Copy), r=[pk], w=[xk])
                pm, ct, st_ = 1, 2, 3
            xs, xsk = P.nxt("ps")
            P.op("pe", lambda e: e.matmul(xs, lhsT=C.perm[:, pm, :], rhs=xb, start=True, stop=True), r=[xk, "perm"], w=[xsk])
            t1, t1k = P.nxt("tf")
            P.op("dve", lambda e: e.tensor_tensor(out=t1, in0=xb, in1=rope[:, ct, ts], op=ALU.mult), r=[xk, "rope"], w=[t1k])
            t2, t2k = P.nxt("tf")
            P.op("dve", lambda e: e.tensor_tensor(out=t2, in0=xs, in1=rope[:, st_, ts], op=ALU.mult), r=[xsk, "rope"], w=[t2k])
            P.op("dve", lambda e: e.tensor_tensor(out=dst, in0=t1, in1=t2, op=ALU.add), r=[t1k, t2k], w=[okey, qk])
        if t == 1:
            C.store(qk_o[si], qsb, [qk], qk, sem=qsm)

    steps = [[(win, 0, ci * 128)] for (ci, kind) in PROJ]
    C.linear(steps, 16, lambda k, t: (xn[:, k, t * 512:(t + 1) * 512], ("xn", k, t)), epi)

    for (c0, ncols, o0) in VGROUPS:
        src = win[:, c0:c0 + ncols].rearrange("(k p) n -> p k n", p=128)
        P.op("pool", lambda e, src=src, ncols=ncols: e.dma_start(out=wv[:, :, 0:ncols], in_=src), w=["wv"], dsem=wv_sem)
        for tb in range(8):
            ps, pk = P.nxt("ps")

            def mm(e, ps=ps, tb=tb, ncols=ncols):
                ins = None
                for k in range(16):
                    ins = e.matmul(ps[:, 0:ncols], lhsT=xn[:, k, tb * 128:(tb + 1) * 128], rhs=wv[:, k, 0:ncols],
                                   start=(k == 0), stop=(k == 15))
                return ins
            P.op("pe", mm, r=["wv"] + [("xn", k, tb // 4) for k in range(16)], w=[pk])
            vsb, vk, vsm = P.nxt("vs")
            P.op("act", lambda e, vsb=vsb, ps=ps, ncols=ncols: e.activation(out=vsb[:, 0:ncols], in_=ps[:, 0:ncols], func=AF.Copy),
                 r=[pk], w=[vk])
            C.store(v_o[tb * 128:(tb + 1) * 128, o0:o0 + ncols], vsb[:, 0:ncols], [vk], vk, sem=vsm)
    P.barrier()
    A.release(m)


def build_stage_b(lam_init, final):
    C = Ctx()
    nc, P, A = C.nc, C.P, C.A
    h_d = C.dram_in("hT", [D, NTOK])
    qA_d = C.dram_in("qA", [4, 128, NTOK], BF16)
    qB_d = C.dram_in("qB", [8, 128, NTOK], BF16)
    qC_d = C.dram_in("qC", [4, 128, NTOK], BF16)
    kB_d = C.dram_in("kB", [2, 128, S], BF16)
    vB_d = C.dram_in("vB", [2, 128, 32, 128], BF16)
    kC_d = C.dram_in("kC", [4, 128, S], BF16)
    vC_d = C.dram_in("vC", [4, 128, 32, 128], BF16)
    kA_d = C.dram_in("kA", [4, 128, 1920], BF16)
    vA_d = C.dram_in("vA", [128, 15, 512], BF16)
    tA_d = C.dram_in("tA", [128, 4, 8, 128])
    mA_d = C.dram_in("mA", [8, 128, 8, 128])
    wo = C.dram_in("w_out", [D, D])
    wg = C.dram_in("w_gate", [D, DFF])
    wu = C.dram_in("w_up", [D, DFF])
    wd = C.dram_in("w_down", [DFF, D])
    wpg = C.dram_in("w_pg", [D, D])
    wpp = C.dram_in("w_pp", [256, D])
    pT_d = C.dram_in("pT", [256, NTOK])
    cv_d = C.dram_in("cv", [128, NCV])
    perm_d = C.dram_in("perm", [2, 128, 128])
    h_o = C.dram_out("hT_out", [D, NTOK])

    C.setup_consts(cv_d, perm_d)
    C.load_h(h_d)
    Oc = C.xn
    emit_attn_a(C, qA_d, kA_d, vA_d, tA_d, mA_d, Oc)
    emit_attn_bc(C, qB_d, kB_d, vB_d, qC_d, kC_d, vC_d, Oc, lam_init)
    h = C.h

    def epi_o(si, t, outs):
        ts = slice(t * 512, (t + 1) * 512)
        (yp, yk), = outs
        P.op("dve", lambda e: e.tensor_tensor(out=h[:, si, ts], in0=yp, in1=h[:, si, ts], op=ALU.add),
             r=[yk, ("h", si)], w=[("h", si)])
    C.linear([[(wo, 0, dc * 128)] for dc in range(16)], 16,
             lambda k, t: (Oc[:, k, t * 512:(t + 1) * 512], ("Oc", k, t)), epi_o)
    P.barrier()
    C.ffn(CV_G2, wg, wu, wd)
    m = A.mark()
    pT = A.alloc((2, NTOK), BF16)
    C.load(pT, pT_d.rearrange("(k p) n -> p k n", p=128), "pT", eng="pool")
    C.rmsnorm(CV_GP, C.xn, "xn")
    xn = C.xn
    for dc in range(16):
        wt1, wk1 = C.wload(wpg, 0, 16, dc * 128)
        wt2, wk2 = C.wload(wpp, 0, 2, dc * 128)
        for t in range(2):
            ts = slice(t * 512, (t + 1) * 512)
            gp, gk = P.nxt("ps")

            def mm1(e, gp=gp, wt1=wt1, ts=ts):
                ins = None
                for k in range(16):
                    ins = e.matmul(gp, lhsT=wt1[:, k, :], rhs=xn[:, k, ts], start=(k == 0), stop=(k == 15))
                return ins
            P.op("pe", mm1, r=[wk1] + [("xn", k, t) for k in range(16)], w=[gk])
            pp, ppk = P.nxt("ps")

            def mm2(e, pp=pp, wt2=wt2, ts=ts):
                ins = None
                for k in range(2):
                    ins = e.matmul(pp, lhsT=wt2[:, k, :], rhs=pT[:, k, ts], start=(k == 0), stop=(k == 1))
                return ins
            P.op("pe", mm2, r=[wk2, "pT"], w=[ppk])
            sg, sk = P.nxt("tf")
            P.op("act", lambda e, sg=sg, gp=gp: e.activation(out=sg, in_=gp, func=AF.Sigmoid), r=[gk], w=[sk])
            P.op("dve", lambda e, sg=sg, pp=pp: e.tensor_tensor(out=sg, in0=pp, in1=sg, op=ALU.mult), r=[ppk, sk], w=[sk])
            P.op("dve", lambda e, sg=sg, dc=dc, ts=ts: e.tensor_tensor(out=h[:, dc, ts], in0=sg, in1=h[:, dc, ts], op=ALU.add),
                 r=[sk, ("h", dc)], w=[("h", dc)])
    P.barrier()
    A.release(m)
    if final:
        m = A.mark()
        of = A.alloc((2, 512), F32)
        ofs = [P.sem("of0"), P.sem("of1")]
        P.ring("of", [(of[:, i, :], ("of", i)) for i in range(2)])

        def st(k, t, ts, o, ok):
            C.store(h_o[k * 128:(k + 1) * 128, ts], o, [ok], ok, sem=ofs[ok[1]])
        C.rmsnorm(CV_GF, None, None, store_fn=st)
    else:
        for k in range(16):
            C.store(h_o[k * 128:(k + 1) * 128, :], C.h[:, k, :], [("h", k)], ("h_o", k))
    finish(C)
    return nc


def emit_attn_a(C, qA_d, kA_d, vA_d, tA_d, mA_d, Oc):
    P, A = C.P, C.A
    m = A.mark()
    kA = A.alloc((4, 1920), BF16)
    vA = A.alloc((15, 512), BF16)
    tA = A.alloc((4, 8, 128), F32)
    qA = A.alloc((4, NTOK), BF16)
    mA = [A.alloc((8, 128), F32) for _ in range(2)]
    tmp = [A.alloc((8, 128), F32) for _ in range(2)]
    pb = [A.alloc((8, 128), BF16) for _ in range(2)]
    msem = [P.sem("mA0"), P.sem("mA1")]
    C.load(kA, kA_d.rearrange("a p n -> p a n"), "kA")
    C.load(vA, vA_d, "vA")
    C.load(tA, tA_d, "tA")
    C.load(qA, qA_d.rearrange("a p n -> p a n"), "qA")

    def body(pi, hh, it, mt, mk):
        qs = slice(pi * 128, (pi + 1) * 128)
        pst = C.pst[it % 2]
        psk = [("ps", (it % 2) * 2), ("ps", (it % 2) * 2 + 1)]
        tp, tpk = tmp[it % 2], ("tmpA", it % 2)
        pp, ppk = pb[it % 2], ("pbA", it % 2)
        acc, acck = C.pst[2 + it % 2], [("ps", 4 + (it % 2) * 2), ("ps", 5 + (it % 2) * 2)]

        def mm(e):
            ins = None
            for j in range(8):
                c0 = (2 * pi + 2 * j) * 64
                ins = e.matmul(pst[:, j * 128:(j + 1) * 128], lhsT=kA[:, hh, c0:c0 + 128],
                               rhs=qA[:, hh, qs], start=True, stop=True)
            return ins
        P.op("pe", mm, r=["kA", "qA"], w=psk)
        P.op("dve", lambda e: e.tensor_tensor(
            out=tp, in0=pst[:, :].rearrange("p (a b) -> p a b", a=8), in1=tA[:, hh, :, :], op=ALU.add),
            r=psk + ["tA"], w=[tpk])
        P.op("dve", lambda e: e.tensor_tensor(out=tp, in0=tp, in1=mt, op=ALU.add), r=[tpk, mk], w=[tpk])
        P.op("act", lambda e: e.activation(out=pp, in_=tp, func=AF.Exp), r=[tpk], w=[ppk])

        def mm2(e):
            ins = None
            for j in range(8):
                ins = e.matmul(acc[:, 0:128], lhsT=vA[:, pi + j, hh * 128:(hh + 1) * 128], rhs=pp[:, j, :],
                               start=(j == 0), stop=(j == 7))
            for j in range(8):
                ins = e.matmul(acc[:, 128:256], lhsT=C.ones, rhs=pp[:, j, :], start=(j == 0), stop=(j == 7))
            return ins
        P.op("pe", mm2, r=[ppk, "vA", "ones"], w=acck)
        rz, rzk = P.nxt("tf")
        P.op("dve", lambda e: e.reciprocal(out=rz[:, 0:128], in_=acc[:, 128:256]), r=acck, w=[rzk])
        P.op("dve", lambda e: e.tensor_tensor(out=Oc[:, hh, qs], in0=acc[:, 0:128], in1=rz[:, 0:128], op=ALU.mult),
             r=acck + [rzk], w=[("Oc", hh, pi // 4)])

    it = 0
    for pi in range(8):
        mt = mA[pi % 2]
        mk = ("mA", pi % 2)
        C.load(mt, mA_d[pi], mk, sem=msem[pi % 2])
        for hh in range(4):
            body(pi, hh, it, mt, mk)
            it += 1
    P.barrier()
    A.release(m)


def emit_attn_bc(C, qB_d, kB_d, vB_d, qC_d, kC_d, vC_d, Oc, lam_init):
    P, A = C.P, C.A
    cv = C.cv
    m = A.mark()
    ks = [A.alloc((S,), BF16) for _ in range(2)]
    vs = [A.alloc((32, 128), BF16) for _ in range(2)]
    kvsem = [P.sem("kv0"), P.sem("kv1")]
    vvsem = [P.sem("vv0"), P.sem("vv1")]
    qs_ = [A.alloc((NTOK,), BF16) for _ in range(3)]
    qsem = [P.sem("q%d" % i) for i in range(3)]
    pts = [A.alloc((512,), BF16) for _ in range(6)]
    P.ring("pt", [(pts[i], ("pt", i)) for i in range(6)])
    sring = [(C.pst[i // 2][:, (i % 2) * 512:(i % 2 + 1) * 512], ("ps", i)) for i in range(4)]
    accs = [(C.pst[2 + i // 2][:, (i % 2) * 512:(i % 2 + 1) * 512], ("ps", 4 + i)) for i in range(4)]
    sc = [0]

    def nxt_s():
        x = sring[sc[0] % 4]
        sc[0] += 1
        return x

    lam = C.misc
    lp = cv[:, CV_LAM:CV_LAM + 256]
    pr, prk = P.nxt("tf")
    P.op("dve", lambda e: e.tensor_tensor(out=pr[:, 0:64], in0=lp[:, 0:64], in1=lp[:, 64:128], op=ALU.mult), r=["cv"], w=[prk])
    P.op("dve", lambda e: e.tensor_tensor(out=pr[:, 64:128], in0=lp[:, 128:192], in1=lp[:, 192:256], op=ALU.mult), r=[prk, "cv"], w=[prk])
    P.op("dve", lambda e: e.reduce_sum(out=lam[:, 0:1], in_=pr[:, 0:64], axis=AX.X), r=[prk], w=["lam"])
    P.op("dve", lambda e: e.reduce_sum(out=lam[:, 1:2], in_=pr[:, 64:128], axis=AX.X), r=[prk, "lam"], w=["lam"])
    P.op("act", lambda e: e.activation(out=lam[:, 2:4], in_=lam[:, 0:2], func=AF.Exp), r=["lam"], w=["lam"])
    P.op("dve", lambda e: e.tensor_tensor(out=lam[:, 4:5], in0=lam[:, 3:4], in1=lam[:, 2:3], op=ALU.subtract), r=["lam"], w=["lam"])
    P.op("dve", lambda e: e.tensor_scalar(out=lam[:, 5:6], in0=lam[:, 4:5], scalar1=-lam_init, scalar2=None, op0=ALU.add), r=["lam"], w=["lam"])
    P.op("dve", lambda e: e.tensor_scalar(out=lam[:, 6:7], in0=cv[:, CV_SUB:CV_SUB + 1], scalar1=1.0 - lam_init, scalar2=None, op0=ALU.mult),
         r=["lam", "cv"], w=["lam"])

    kvi = [0]
    qi = [0]

    def load_kv(k_src, v_src):
        i = kvi[0] % 2
        kvi[0] += 1
        kk, vk = ("kslot", i), ("vslot", i)
        C.load(ks[i], k_src, kk, sem=kvsem[i])
        C.load(vs[i], v_src, vk, sem=vvsem[i])
        return ks[i], vs[i], kk, vk

    def load_q(src):
        i = qi[0] % 3
        qi[0] += 1
        C.load(qs_[i], src, ("qslot", i), sem=qsem[i])
        return qs_[i], ("qslot", i)

    scaleB = 128.0 ** -0.5
    scaleC = 64.0 ** -0.5

    def do_b(hd, t, K, V, kk, vk, q, qk):
        ts = slice(t * 512, (t + 1) * 512)
        x = (hd * 2 + t) % 2
        (oa, oak), (za, zak) = accs[x * 2], accs[x * 2 + 1]

        def s_mm(kb):
            sp_, spk = nxt_s()
            P.op("pe", lambda e: e.matmul(sp_, lhsT=K[:, kb * 128:(kb + 1) * 128], rhs=q[:, ts], start=True, stop=True),
                 r=[kk, qk], w=[spk])
            return sp_, spk

        def step(kb, sp_, spk):
            pt, ptk = P.nxt("pt")
            P.op("act", lambda e: e.activation(out=pt, in_=sp_, func=AF.Exp, scale=scaleB), r=[spk], w=[ptk])

            def pv(e):
                e.matmul(oa, lhsT=V[:, kb, :], rhs=pt, start=(kb == 0), stop=(kb == 31))
                return e.matmul(za, lhsT=C.ones, rhs=pt, start=(kb == 0), stop=(kb == 31))
            P.op("pe", pv, r=[ptk, vk, "ones"], w=[oak, zak])
        nxt = s_mm(0)
        for kb in range(32):
            cur = nxt
            if kb + 1 < 32:
                nxt = s_mm(kb + 1)
            step(kb, *cur)
        rz, rzk = P.nxt("tf")
        P.op("dve", lambda e: e.reciprocal(out=rz, in_=za), r=[zak], w=[rzk])
        P.op("dve", lambda e: e.tensor_tensor(out=Oc[:, 4 + hd, ts], in0=oa, in1=rz, op=ALU.mult),
             r=[oak, rzk], w=[("Oc", 4 + hd, t)])

    for g in range(2):
        K, V, kk, vk = load_kv(kB_d[g], vB_d[g])
        for hq in range(4):
            hd = g * 4 + hq
            q, qk = load_q(qB_d[hd])
            for t in range(2):
                do_b(hd, t, K, V, kk, vk, q, qk)

    def do_c(hh, t, K, V, kk, vk, q, qk):
        ts = slice(t * 512, (t + 1) * 512)
        (o1, o1k), (z1, z1k), (o2, o2k), (z2, z2k) = accs

        def s_mm(kb):
            s1, s1k = nxt_s()
            s2, s2k = nxt_s()

            def f(e):
                e.matmul(s1, lhsT=K[0:64, kb * 128:(kb + 1) * 128], rhs=q[0:64, ts], start=True, stop=True)
                return e.matmul(s2, lhsT=K[64:128, kb * 128:(kb + 1) * 128], rhs=q[64:128, ts], start=True, stop=True)
            P.op("pe", f, r=[kk, qk], w=[s1k, s2k])
            return s1, s1k, s2, s2k

        def step(kb, s1, s1k, s2, s2k):
            p1, p1k = P.nxt("pt")
            p2, p2k = P.nxt("pt")
            P.op("act", lambda e: e.activation(out=p1, in_=s1, func=AF.Exp, scale=scaleC), r=[s1k], w=[p1k])
            P.op("act", lambda e: e.activation(out=p2, in_=s2, func=AF.Exp, scale=scaleC), r=[s2k], w=[p2k])

            def pv(e):
                st, sp = (kb == 0), (kb == 31)
                e.matmul(o1, lhsT=V[:, kb, :], rhs=p1, start=st, stop=sp)
                e.matmul(z1, lhsT=C.ones, rhs=p1, start=st, stop=sp)
                e.matmul(o2, lhsT=V[:, kb, :], rhs=p2, start=st, stop=sp)
                return e.matmul(z2, lhsT=C.ones, rhs=p2, start=st, stop=sp)
            P.op("pe", pv, r=[p1k, p2k, vk, "ones"], w=[o1k, z1k, o2k, z2k])
        nxt = s_mm(0)
        for kb in range(32):
            cur = nxt
            if kb + 1 < 32:
                nxt = s_mm(kb + 1)
            step(kb, *cur)
        r1, r1k = P.nxt("tf")
        r2, r2k = P.nxt("tf")
        P.op("dve", lambda e: e.reciprocal(out=r1, in_=z1), r=[z1k], w=[r1k])
        P.op("dve", lambda e: e.reciprocal(out=r2, in_=z2), r=[z2k], w=[r2k])
        P.op("dve", lambda e: e.tensor_tensor(out=r1, in0=o1, in1=r1, op=ALU.mult), r=[o1k, r1k], w=[r1k])
        P.op("dve", lambda e: e.tensor_tensor(out=r2, in0=o2, in1=r2, op=ALU.mult), r=[o2k, r2k], w=[r2k])
        od, odk = P.nxt("tf")
        P.op("dve", lambda e: e.scalar_tensor_tensor(out=od, in0=r2, scalar=lam[:, 5:6], in1=r1, op0=ALU.mult, op1=ALU.add),
             r=[r1k, r2k, "lam"], w=[odk])
        sq, sqk = P.nxt("tb")
        P.op("act", lambda e: e.activation(out=sq, in_=od, func=AF.Square), r=[odk], w=[sqk])
        ss, ssk = nxt_s()
        P.op("pe", lambda e: e.matmul(ss, lhsT=C.ones, rhs=sq, start=True, stop=True), r=[sqk, "ones"], w=[ssk])
        rs, rk = C.rstd_from_ss(ss, ssk, 128)
        P.op("dve", lambda e: e.scalar_tensor_tensor(out=Oc[:, 12 + hh, ts], in0=od, scalar=lam[:, 6:7], in1=rs,
                                                     op0=ALU.mult, op1=ALU.mult),
             r=[odk, rk, "lam"], w=[("Oc", 12 + hh, t)])

    for hh in range(4):
        K, V, kk, vk = load_kv(kC_d[hh], vC_d[hh])
        q, qk = load_q(qC_d[hh])
        for t in range(2):
            do_c(hh, t, K, V, kk, vk, q, qk)
    P.barrier()
    A.release(m)


_CACHE = {}


def _bf(x):
    return np.ascontiguousarray(x).astype(ml_dtypes.bfloat16) if x.dtype != ml_dtypes.bfloat16 else np.ascontiguousarray(x)


def _rope_tables():
    f32 = np.float32
    t = np.arange(S)
    inv_ax = np.power(f32(10000.0), -np.arange(0, 64, 2, dtype=f32) / f32(64)).astype(f32)
    ang_b = np.concatenate([(t // 64).astype(f32)[:, None] * inv_ax, (t % 64).astype(f32)[:, None] * inv_ax], axis=-1)
    cos_b, sin_b = np.cos(ang_b).astype(f32), np.sin(ang_b).astype(f32)
    inv_1d = np.power(f32(10000.0), -np.arange(0, 64, 2, dtype=f32) / f32(64)).astype(f32)
    ang_c = t.astype(f32)[:, None] * inv_1d
    cos_c, sin_c = np.cos(ang_c).astype(f32), np.sin(ang_c).astype(f32)
    cB = np.concatenate([cos_b, cos_b], axis=1).T
    sB = np.concatenate([-sin_b, sin_b], axis=1).T
    cC = np.concatenate([cos_c, cos_c, cos_c, cos_c], axis=1).T
    sC = np.concatenate([-sin_c, sin_c, -sin_c, sin_c], axis=1).T
    tab = np.stack([cB, sB, cC, sC]).astype(f32)
    perm = np.zeros((2, 128, 128), f32)
    for mcol in range(128):
        perm[0, (mcol + 64) % 128, mcol] = 1.0
        pm = mcol + 32 if (mcol % 64) < 32 else mcol - 32
        perm[1, pm, mcol] = 1.0
    return tab, perm


def _na_mask(rank):
    r0 = rank * 16
    out = np.full((8, 128, 8, 128), NEG, np.float32)
    for pi in range(8):
        for j in range(8):
            for a in range(2):
                for b in range(2):
                    r = r0 + 2 * pi + b
                    rs = min(max(r - 4, 0), 56)
                    kr = r0 - 7 + 2 * pi + 2 * j + a
                    if rs <= kr < rs + 8:
                        out[pi, a * 64:(a + 1) * 64, j, b * 64:(b + 1) * 64] = 0.0
    return out


def _na_bias_table(rpb):
    kc = np.arange(64)[:, None]
    qc = np.arange(64)[None, :]
    cs = np.clip(qc - 8, 0, 48)
    colok = (kc >= cs) & (kc < cs + 16)
    dc = np.clip(kc - qc + 15, 0, 30)
    out = np.full((128, 4, 8, 128), NEG, np.float32)
    for hh in range(4):
        for j in range(8):
            for a in range(2):
                for b in range(2):
                    dr = 2 * j - 7 + a - b
                    if -7 <= dr <= 7:
                        blk = np.where(colok, rpb[hh, dr + 7][dc], np.float32(NEG))
                        out[a * 64:(a + 1) * 64, hh, j, b * 64:(b + 1) * 64] = blk
    return out


def _cvec(inp, i):
    cv = np.zeros((128, NCV), np.float32)
    for col, nm in ((CV_G1, "g_ffn1"), (CV_GM, "g_mix"), (CV_G2, "g_ffn2"), (CV_GP, "g_ple")):
        cv[:, col:col + 16] = inp[nm][i].reshape(16, 128).T
    cv[:, CV_GF:CV_GF + 16] = inp["g_final"].reshape(16, 128).T
    cv[:, CV_QG] = inp["gqa_q_gain"][i]
    cv[:, CV_KG] = inp["gqa_k_gain"][i]
    cv[:, CV_SUB] = inp["diff_subln_gain"][i]
    cv[:, CV_LAM:CV_LAM + 256] = inp["diff_lambda"][i].reshape(1, 256)
    return cv


def _get(name, fn):
    if name not in _CACHE:
        _CACHE[name] = fn()
    return _CACHE[name]


def _b_inputs(inp, i, c, hT, qk, vv, tA, masks, cv, perm):
    b, r = c // 4, c % 4
    grp = [b * 4 + j for j in range(4)]
    qkf = np.concatenate([qk[j] for j in grp], axis=2)
    vf = np.concatenate([vv[j] for j in grp], axis=0)
    r0 = r * 16
    kA = np.zeros((4, 128, 30 * 64), ml_dtypes.bfloat16)
    vA = np.zeros((30 * 64, 512), ml_dtypes.bfloat16)
    lo, hi = max(r0 - 7, 0), min(r0 + 23, 64)
    kA[:, :, (lo - (r0 - 7)) * 64:(hi - (r0 - 7)) * 64] = qkf[4:8, :, lo * 64:hi * 64]
    vA[(lo - (r0 - 7)) * 64:(hi - (r0 - 7)) * 64] = vf[lo * 64:hi * 64, 0:512]
    vA = np.ascontiguousarray(vA.reshape(15, 128, 512).transpose(1, 0, 2))
    vB = np.ascontiguousarray(vf[:, 512:768].reshape(32, 128, 2, 128).transpose(2, 1, 0, 3))
    vC = np.ascontiguousarray(vf[:, 768:1280].reshape(32, 128, 4, 128).transpose(2, 1, 0, 3))
    return {
        "hT": hT[c], "qA": np.ascontiguousarray(qk[c][0:4]), "qB": np.ascontiguousarray(qk[c][8:16]),
        "qC": np.ascontiguousarray(qk[c][18:22]),
        "kB": np.ascontiguousarray(qkf[16:18]), "vB": vB, "kC": np.ascontiguousarray(qkf[22:26]), "vC": vC,
        "kA": kA, "vA": vA, "tA": tA, "mA": masks[r],
        "w_out": inp["w_out"][i], "w_gate": inp["w2_gate"][i], "w_up": inp["w2_up"][i], "w_down": inp["w2_down"][i],
        "w_pg": inp["w_ple_gate"][i], "w_pp": inp["w_ple_proj"][i],
        "pT": np.ascontiguousarray(inp["p"][i, b, r * NTOK:(r + 1) * NTOK, :].T),
        "cv": cv, "perm": perm}


def kernel(**inp):
    inp = {k: np.asarray(v) for k, v in inp.items()}
    tab, perm = _rope_tables()
    cores = list(range(8))
    x = inp["x"]
    hT = [np.ascontiguousarray(x[c // 4, (c % 4) * NTOK:(c % 4 + 1) * NTOK, :].T) for c in cores]
    rope_c = [np.ascontiguousarray(tab[:, :, (c % 4) * NTOK:(c % 4 + 1) * NTOK]) for c in cores]
    masks = [_na_mask(r) for r in range(4)]
    nca = _get("a", build_stage_a)
    for i in range(2):
        cv = _cvec(inp, i)
        in_maps = [{"hT": hT[c], "w_gate": inp["w1_gate"][i], "w_up": inp["w1_up"][i], "w_down": inp["w1_down"][i],
                    "w_in": inp["w_in"][i], "cv": cv, "perm": perm, "rope": rope_c[c]} for c in cores]
        res = run_bass_kernel_spmd(nca, in_maps, core_ids=cores).results
        hT = [res[c]["hT_out"] for c in cores]
        qk = [res[c]["qk_out"] for c in cores]
        vv = [res[c]["v_out"] for c in cores]
        tA = _na_bias_table(inp["na_rpb"][i])
        lam_init = 0.8 - 0.6 * math.exp(-0.3 * i)
        ncb = _get(("b", i), lambda: build_stage_b(lam_init, i == 1))
        in_maps = [_b_inputs(inp, i, c, hT, qk, vv, tA, masks, cv, perm) for c in cores]
        res = run_bass_kernel_spmd(ncb, in_maps, core_ids=cores).results
        hT = [res[c]["hT_out"] for c in cores]
    out = np.zeros((2, S, D), np.float32)
    for c in cores:
        out[c // 4, (c % 4) * NTOK:(c % 4 + 1) * NTOK, :] = hT[c].T
    return out
```
